# Optimizing a Trainium2 kernel written in Bass

```python
import jax, jax.numpy as jnp
from jax import lax
import numpy as np

D_MODEL = 1024
BATCH = 32
SEQ = 2048
DEPTH = 4
DEC_BATCH = 16
DEC_SEQ = 16
PAST_LEN = 4096

CHUNK = 64
N_MIXERS = 4
BRANCH = D_MODEL
EPS = 1e-6
POOL_WINDOWS = (2, 4, 8, 16)
N_POOL_GROUPS = len(POOL_WINDOWS)
POOL_GROUP = BRANCH // N_POOL_GROUPS
POOL_STATE = max(POOL_WINDOWS) - 1
CONV_WIDTH = 31
CONV_STATE = CONV_WIDTH - 1
SB_HEADS = 8
SB_HEAD_DIM = BRANCH // SB_HEADS
Q_BLOCK = 128
MLP_CHUNK = 128
MLP_GROUPS = 4
MLP_GROUP = BRANCH // MLP_GROUPS
N_POOL_LAYERS = (DEPTH + 3) // 4
N_CONV_LAYERS = (DEPTH + 2) // 4
N_SB_LAYERS = (DEPTH + 1) // 4
N_GMLP_LAYERS = DEPTH // 4

kernel_name = "hybrid_streaming_encoder_step"


def _rmsnorm(x, g):
    x32 = x.astype(jnp.float32)
    y = x32 * lax.rsqrt(jnp.mean(x32 * x32, axis=-1, keepdims=True) + EPS)
    return (y * g.astype(jnp.float32)).astype(x.dtype)


def _layernorm(x, g, b):
    x32 = x.astype(jnp.float32)
    mu = jnp.mean(x32, axis=-1, keepdims=True)
    xc = x32 - mu
    var = jnp.mean(xc * xc, axis=-1, keepdims=True)
    y = xc * lax.rsqrt(var + EPS) * g.astype(jnp.float32) + b.astype(jnp.float32)
    return y.astype(x.dtype)


def _pool_mixer(u, buf, start_pos, w_grp, scale):
    bsz, t_len, _ = u.shape
    full = jnp.concatenate([buf.astype(u.dtype), u], axis=1)
    csum = jnp.cumsum(full.astype(jnp.float32), axis=1)
    csum = jnp.pad(csum, ((0, 0), (1, 0), (0, 0)))
    pos = start_pos + jnp.arange(t_len)
    hi = csum[:, POOL_STATE + 1:]
    means = []
    for gi, w in enumerate(POOL_WINDOWS):
        sl = slice(gi * POOL_GROUP, (gi + 1) * POOL_GROUP)
        lo = csum[:, POOL_STATE + 1 - w: POOL_STATE + 1 - w + t_len, sl]
        cnt = jnp.minimum(pos + 1, w).astype(jnp.float32)[None, :, None]
        means.append((hi[..., sl] - lo) / cnt)
    pooled = (jnp.concatenate(means, axis=-1) - u.astype(jnp.float32)).astype(u.dtype)
    pooled = pooled.reshape(bsz, t_len, N_POOL_GROUPS, POOL_GROUP)
    y = jnp.einsum('btgc,gcd->btgd', pooled, w_grp).reshape(bsz, t_len, BRANCH)
    return y * scale, full[:, -POOL_STATE:]


def _conv_module(a, b, buf, w_dw, b_dw, ln_g, ln_b):
    h = a * jax.nn.sigmoid(b)
    full = jnp.concatenate([buf.astype(h.dtype), h], axis=1)
    y = lax.conv_general_dilated(full, w_dw[:, None, :].astype(h.dtype), (1,), 'VALID',
                                 dimension_numbers=('NWC', 'WIO', 'NWC'),
                                 feature_group_count=BRANCH) + b_dw
    y = jax.nn.silu(_layernorm(y, ln_g, ln_b))
    return y, full[:, -CONV_STATE:]


def _sb_block(q, k, v, q_pos, k_pos):
    z = jnp.einsum('bqhd,bkhd->bhqk', q, k).astype(jnp.float32) * (SB_HEAD_DIM ** -0.5)
    mask = (k_pos[None, :] < q_pos[:, None])[None, None]
    log_beta = jax.nn.log_sigmoid(z)
    log_1m = jnp.where(mask, jax.nn.log_sigmoid(-z), 0.0)
    suffix = jnp.pad(lax.cumsum(log_1m, axis=3, reverse=True)[..., 1:],
                     ((0, 0), (0, 0), (0, 0), (0, 1)))
    attn = jnp.where(mask, jnp.exp(log_beta + suffix), 0.0)
    return jnp.einsum('bhqk,bkhd->bqhd', attn.astype(v.dtype), v)


def _sb_attend(q, k, v, q_offset):
    t_len = q.shape[1]
    outs = []
    for s0 in range(0, t_len, Q_BLOCK):
        s1 = min(s0 + Q_BLOCK, t_len)
        n_keys = q_offset + s1
        outs.append(_sb_block(q[:, s0:s1], k[:, :n_keys], v[:, :n_keys],
                              q_offset + jnp.arange(s0, s1), jnp.arange(n_keys)))
    return jnp.concatenate(outs, axis=1)


def _sgu(u, v, w_s, b_s):
    bsz, t_len, _ = v.shape
    length = min(t_len, MLP_CHUNK)
    n_chunks = t_len // length
    mask = jnp.tril(jnp.ones((length, length), dtype=bool))
    w = jnp.where(mask, w_s[:, :length, :length], 0.0)
    v5 = v.reshape(bsz, n_chunks, length, MLP_GROUPS, MLP_GROUP)
    mixed = jnp.einsum('gts,bnsgc->bntgc', w, v5) + b_s[:, :length].T[None, None, :, :, None]
    return u * mixed.reshape(bsz, t_len, BRANCH)


def setup_inputs(seed: int = 0) -> dict:
    key = jax.random.key(seed)
    ks = iter(jax.random.split(key, 40))

    def nrm(shape, scale):
        return jax.random.normal(next(ks), shape, jnp.float32) * scale

    d, e = D_MODEL, BRANCH
    return {
        "x_prompt": nrm((BATCH, SEQ, d), 1.0),
        "x_sample": nrm((DEC_BATCH, DEC_SEQ, d), 1.0),
        "cache_pool": nrm((N_POOL_LAYERS, DEC_BATCH, POOL_STATE, e), 1.0),
        "cache_conv": nrm((N_CONV_LAYERS, DEC_BATCH, CONV_STATE, e), 0.5),
        "cache_sb_k": nrm((N_SB_LAYERS, DEC_BATCH, PAST_LEN, SB_HEADS, SB_HEAD_DIM), 1.0),
        "cache_sb_v": nrm((N_SB_LAYERS, DEC_BATCH, PAST_LEN, SB_HEADS, SB_HEAD_DIM), 1.0),
        "norm_g": 1.0 + nrm((DEPTH, d), 0.05),
        "final_g": 1.0 + nrm((d,), 0.05),
        "a_w_in": nrm((N_POOL_LAYERS, d, 2 * e), d ** -0.5),
        "a_w_grp": nrm((N_POOL_LAYERS, N_POOL_GROUPS, POOL_GROUP, POOL_GROUP), POOL_GROUP ** -0.5),
        "a_scale": 1.0 + nrm((N_POOL_LAYERS, e), 0.1),
        "a_w_out": nrm((N_POOL_LAYERS, e, d), e ** -0.5),
        "b_w_in": nrm((N_CONV_LAYERS, d, 3 * e), d ** -0.5),
        "b_w_dw": nrm((N_CONV_LAYERS, CONV_WIDTH, e), CONV_WIDTH ** -0.5),
        "b_b_dw": nrm((N_CONV_LAYERS, e), 0.02),
        "b_ln_g": 1.0 + nrm((N_CONV_LAYERS, e), 0.05),
        "b_ln_b": nrm((N_CONV_LAYERS, e), 0.02),
        "b_w_out": nrm((N_CONV_LAYERS, e, d), e ** -0.5),
        "c_w_in": nrm((N_SB_LAYERS, d, 4 * e), d ** -0.5),
        "c_q_g": 1.0 + nrm((N_SB_LAYERS, SB_HEAD_DIM), 0.05),
        "c_k_g": 1.0 + nrm((N_SB_LAYERS, SB_HEAD_DIM), 0.05),
        "c_w_out": nrm((N_SB_LAYERS, e, d), e ** -0.5),
        "d_w_in": nrm((N_GMLP_LAYERS, d, 3 * e), d ** -0.5),
        "d_v_g": 1.0 + nrm((N_GMLP_LAYERS, e), 0.05),
        "d_w_s": nrm((N_GMLP_LAYERS, MLP_GROUPS, MLP_CHUNK, MLP_CHUNK), MLP_CHUNK ** -0.5),
        "d_b_s": 1.0 + nrm((N_GMLP_LAYERS, MLP_GROUPS, MLP_CHUNK), 0.05),
        "d_w_out": nrm((N_GMLP_LAYERS, e, d), e ** -0.5),
    }


def reference(x_prompt, x_sample, cache_pool, cache_conv, cache_sb_k, cache_sb_v,
              norm_g, final_g,
              a_w_in, a_w_grp, a_scale, a_w_out,
              b_w_in, b_w_dw, b_b_dw, b_ln_g, b_ln_b, b_w_out,
              c_w_in, c_q_g, c_k_g, c_w_out,
              d_w_in, d_v_g, d_w_s, d_b_s, d_w_out):

    def run(x, pool_bufs, conv_bufs, sb_ks, sb_vs, start):
        bsz, t_len, _ = x.shape
        new_pool, new_conv, new_k, new_v, new_gv = [], [], [], [], []
        for i in range(DEPTH):
            kind, j = i % N_MIXERS, i // N_MIXERS
            h = _rmsnorm(x, norm_g[i])
            if kind == 0:
                u, gate = jnp.split(h @ a_w_in[j], 2, axis=-1)
                y, buf = _pool_mixer(u, pool_bufs[j], start, a_w_grp[j], a_scale[j])
                new_pool.append(buf)
                w_out = a_w_out[j]
            elif kind == 1:
                ga, gb, gate = jnp.split(h @ b_w_in[j], 3, axis=-1)
                y, buf = _conv_module(ga, gb, conv_bufs[j], b_w_dw[j], b_b_dw[j], b_ln_g[j], b_ln_b[j])
                new_conv.append(buf)
                w_out = b_w_out[j]
            elif kind == 2:
                q, k, v, gate = jnp.split(h @ c_w_in[j], 4, axis=-1)
                hs = (bsz, t_len, SB_HEADS, SB_HEAD_DIM)
                q = _rmsnorm(q.reshape(hs), c_q_g[j])
                k = _rmsnorm(k.reshape(hs), c_k_g[j])
                v = v.reshape(hs)
                if sb_ks is None:
                    k_all, v_all = k, v
                else:
                    k_all = jnp.concatenate([sb_ks[j].astype(k.dtype), k], axis=1)
                    v_all = jnp.concatenate([sb_vs[j].astype(v.dtype), v], axis=1)
                y = _sb_attend(q, k_all, v_all, start).reshape(bsz, t_len, BRANCH)
                new_k.append(k)
                new_v.append(v)
                w_out = c_w_out[j]
            else:
                u, vv, gate = jnp.split(h @ d_w_in[j], 3, axis=-1)
                vv = _rmsnorm(vv, d_v_g[j])
                y = _sgu(u, vv, d_w_s[j], d_b_s[j])
                new_gv.append(vv)
                w_out = d_w_out[j]
            x = x + (y * jax.nn.silu(gate)) @ w_out
        return _rmsnorm(x, final_g), new_pool, new_conv, new_k, new_v, new_gv

    zero_pool = jnp.zeros((N_POOL_LAYERS, x_prompt.shape[0], POOL_STATE, BRANCH), x_prompt.dtype)
    zero_conv = jnp.zeros((N_CONV_LAYERS, x_prompt.shape[0], CONV_STATE, BRANCH), x_prompt.dtype)
    y_prompt, p_pool, p_conv, p_k, p_v, _ = run(x_prompt, zero_pool, zero_conv, None, None, 0)
    y_sample, s_pool, s_conv, s_k, s_v, s_gv = run(x_sample, cache_pool, cache_conv,
                                                  cache_sb_k, cache_sb_v, PAST_LEN)
    return (y_prompt, y_sample,
            jnp.stack(p_pool), jnp.stack(s_pool),
            jnp.stack(p_conv), jnp.stack(s_conv),
            jnp.stack(p_k), jnp.stack(p_v),
            jnp.stack(s_k), jnp.stack(s_v),
            jnp.stack(s_gv))
```

```python
import numpy as np
import ml_dtypes
from contextlib import ExitStack
import concourse.bass as bass
import concourse.mybir as mybir
from concourse.bass_utils import run_bass_kernel_spmd

F32 = mybir.dt.float32
BF16 = mybir.dt.bfloat16
ALU = mybir.AluOpType
ACTF = mybir.ActivationFunctionType

D = 1024
NCH = 8
EPS = 1e-6
NEG = -30000.0


class Buf:
    __slots__ = ("name", "lw", "rd")

    def __init__(self, name):
        self.name = name
        self.lw = None
        self.rd = []


class Sched:
    ENG = ("pe", "act", "dve", "pool", "sp")

    def __init__(self, nc):
        self.nc = nc
        self.ops = {e: [] for e in self.ENG}
        self.ndsem = 0
        self.dcount = []
        self.out_events = []
        self._dsem_of = {}
        self.pending_dma = []

    def _deps(self, eng, reads, writes):
        deps = []
        for b in reads:
            if b.lw is not None:
                deps.append(b.lw)
        for b in writes:
            if b.lw is not None:
                deps.append(b.lw)
            for ev in b.rd:
                if ev[0] == 'e' and ev[1] == eng:
                    continue
                deps.append(ev)
        if eng == "pe":
            deps = [d for d in deps if not (d[0] == 'e' and d[1] == 'pe')]
        return deps

    def _commit(self, ev, reads, writes):
        for b in reads:
            if b in writes:
                continue
            if ev[0] == 'e':
                b.rd = [r for r in b.rd if not (r[0] == 'e' and r[1] == ev[1])]
            else:
                b.rd = [r for r in b.rd if not (r[0] == 'd' and r[1] == ev[1])]
            b.rd.append(ev)
        for b in writes:
            b.lw = ev
            b.rd = []

    def _mark(self, deps):
        for d in deps:
            if d[0] == 'e':
                self.ops[d[1]][d[2]]['signal'] = True

    def op(self, eng, fn, reads=(), writes=()):
        reads = list(reads)
        writes = list(writes)
        deps = self._deps(eng, reads, writes)
        idx = len(self.ops[eng])
        self.ops[eng].append(dict(fn=fn, deps=deps, signal=False, dma=None))
        self._mark(deps)
        self._commit(('e', eng, idx), reads, writes)
        return idx

    def dma(self, out, in_, reads=(), writes=(), q="sp", is_output=False, **kw):
        reads = list(reads)
        writes = list(writes)
        deps = self._deps(None, reads, writes)
        key = writes[0] if writes else reads[0]
        kk = (id(key), bool(writes))
        if kk not in self._dsem_of:
            self._dsem_of[kk] = self.ndsem
            self.ndsem += 1
            self.dcount.append(0)
        s = self._dsem_of[kk]
        self.dcount[s] += 16
        ev = ('d', s, self.dcount[s])
        self.ops[q].append(dict(fn=None, deps=deps, signal=False, dma=(out, in_, s, kw)))
        self._mark(deps)
        self._commit(ev, reads, writes)
        self.pending_dma.append(ev)
        if is_output:
            self.out_events.append(ev)
        return ev

    def barrier(self):
        last = {}
        for e in self.ENG:
            if e == "sp":
                continue
            if self.ops[e]:
                last[e] = ('e', e, len(self.ops[e]) - 1)
        dm = {}
        for ev in self.pending_dma:
            dm[ev[1]] = max(dm.get(ev[1], 0), ev[2])
        dmev = [('d', k, v) for k, v in dm.items()]
        self.pending_dma = []
        for e in self.ENG:
            deps = [v for k, v in last.items() if k != e] + dmev
            self._mark(deps)
            self.ops[e].append(dict(fn=(lambda h: h.nop()), deps=deps, signal=False, dma=None))

    def emit(self, stack):
        nc = self.nc
        esem = {e: stack.enter_context(nc.semaphore("s_" + e)) for e in self.ENG}
        dsem = [stack.enter_context(nc.semaphore("d%d" % i)) for i in range(self.ndsem)]
        cnt = {}
        for e in self.ENG:
            c = 0
            arr = []
            for o in self.ops[e]:
                if o['signal']:
                    c += 1
                arr.append(c)
            cnt[e] = arr
        block = stack.enter_context(nc.Block())
        sched = self

        def run(e, h):
            seen_e = {x: 0 for x in sched.ENG}
            seen_d = {}
            for o in sched.ops[e]:
                need_e = {}
                need_d = {}
                for d in o['deps']:
                    if d[0] == 'e':
                        v = cnt[d[1]][d[2]]
                        if v > seen_e[d[1]]:
                            need_e[d[1]] = max(need_e.get(d[1], 0), v)
                    else:
                        if d[2] > seen_d.get(d[1], 0):
                            need_d[d[1]] = max(need_d.get(d[1], 0), d[2])
                for k, v in need_e.items():
                    h.wait_ge(esem[k], v)
                    seen_e[k] = v
                for k, v in need_d.items():
                    h.wait_ge(dsem[k], v)
                    seen_d[k] = v
                if o['dma'] is not None:
                    out, in_, s, kw = o['dma']
                    h.dma_start(out=out, in_=in_, **kw).then_inc(dsem[s], 16)
                else:
                    ins = o['fn'](h)
                    if o['signal']:
                        ins.then_inc(esem[e], 1)
            if e == "sp":
                for ev in sched.out_events:
                    if ev[2] > seen_d.get(ev[1], 0):
                        h.wait_ge(dsem[ev[1]], ev[2])
                        seen_d[ev[1]] = ev[2]

        @block.tensor
        def _(h):
            run("pe", h)

        @block.scalar
        def _(h):
            run("act", h)

        @block.vector
        def _(h):
            run("dve", h)

        @block.gpsimd
        def _(h):
            run("pool", h)

        @block.sync
        def _(h):
            run("sp", h)


def host_consts():
    bf = ml_dtypes.bfloat16
    cb = np.zeros((128, 5 * 128 + 4 * 512), np.float32)
    cb[:, 0:128] = np.eye(128)
    cb[:, 128:256] = 1.0 / 1024.0
    cb[:, 256:384] = 1.0 / 128.0
    j = np.arange(128)[:, None]
    s = np.arange(128)[None, :]
    cb[:, 384:512] = -(j >= s).astype(np.float32)
    cb[:, 512:640] = -1.0
    q = np.arange(512)[None, :]
    for jm in range(4):
        cb[:, 640 + jm * 512: 640 + (jm + 1) * 512] = np.where(jm * 128 + j < q, 0.0, NEG)
    cf = np.zeros((128, 128 + 128 + 16), np.float32)
    cf[:, 0:128] = np.eye(128)
    cf[:, 128:256] = (j <= s).astype(np.float32)
    cf[:, 256:271] = 1.0 / (np.arange(15)[None, :] + 1.0)
    return cb.astype(bf), cf


def build(NP, TP, NS, TS, PAST, layers=(0, 1, 2, 3), do_final=True):
    nc = bass.Bass("TRN2", target_bir_lowering=False)
    TMAX = max(TP if NP else 0, TS if NS else 0, 2048)

    def din(name, shape, dt=F32):
        return nc.dram_tensor(name, list(shape), dt, kind="ExternalInput").ap()

    def dout(name, shape):
        return nc.dram_tensor(name, list(shape), F32, kind="ExternalOutput").ap()

    I = {}
    I["xp"] = din("xp", [max(NP, 1), TP, D])
    I["xs"] = din("xs", [max(NS, 1), TS, D])
    I["cpool"] = din("cpool", [max(NS, 1), 15, D])
    I["cconv"] = din("cconv", [max(NS, 1), 30, D])
    I["ck"] = din("ck", [max(NS, 1), PAST, D])
    I["cv"] = din("cv", [max(NS, 1), PAST, D])
    I["norm_g"] = din("norm_g", [4, D])
    I["final_g"] = din("final_g", [D])
    I["a_w_in"] = din("a_w_in", [D, 2 * D])
    I["a_w_grp"] = din("a_w_grp", [4, 256, 256])
    I["a_scale"] = din("a_scale", [D])
    I["a_w_out"] = din("a_w_out", [D, D])
    I["b_w_in"] = din("b_w_in", [D, 3 * D])
    I["b_w_dw"] = din("b_w_dw", [31, D])
    I["b_b_dw"] = din("b_b_dw", [D])
    I["b_ln_g"] = din("b_ln_g", [D])
    I["b_ln_b"] = din("b_ln_b", [D])
    I["b_w_out"] = din("b_w_out", [D, D])
    I["c_w_in"] = din("c_w_in", [D, 4 * D])
    I["c_q_g"] = din("c_q_g", [128])
    I["c_k_g"] = din("c_k_g", [128])
    I["c_w_out"] = din("c_w_out", [D, D])
    I["d_w_in"] = din("d_w_in", [D, 3 * D])
    I["d_v_g"] = din("d_v_g", [D])
    I["d_w_s"] = din("d_w_s", [4, 128, 128])
    I["d_b_s"] = din("d_b_s", [4, 128])
    I["d_w_out"] = din("d_w_out", [D, D])
    CBW = 5 * 128 + 4 * 512
    I["cb"] = din("cb", [128, CBW], BF16)
    I["cf"] = din("cf", [128, 272])
    O = {}
    O["yp"] = dout("yp", [max(NP, 1), TP, D])
    O["ys"] = dout("ys", [max(NS, 1), TS, D])
    O["poolp"] = dout("poolp", [max(NP, 1), 15, D])
    O["pools"] = dout("pools", [max(NS, 1), 15, D])
    O["convp"] = dout("convp", [max(NP, 1), 30, D])
    O["convs"] = dout("convs", [max(NS, 1), 30, D])
    O["kp"] = dout("kp", [max(NP, 1), TP, D])
    O["vp"] = dout("vp", [max(NP, 1), TP, D])
    O["ks"] = dout("ks", [max(NS, 1), TS, D])
    O["vs"] = dout("vs", [max(NS, 1), TS, D])
    O["gvs"] = dout("gvs", [max(NS, 1), TS, D])

    st = ExitStack()
    S = Sched(nc)
    _bufs = {}

    def B(name):
        if name not in _bufs:
            _bufs[name] = Buf(name)
        return _bufs[name]

    def sb(name, shape, dt):
        return st.enter_context(nc.sbuf_tensor(name, list(shape), dt))

    X = sb("X", [128, NCH, TMAX], F32)
    HB = sb("HB", [128, NCH, TMAX], BF16)
    MB = sb("MB", [128, NCH, TMAX], BF16)
    NRING = 3
    WBLK = [sb("WBLK%d" % i, [128, NCH, 256], BF16) for i in range(NRING)]
    CB = sb("CB", [128, CBW], BF16)
    CF = sb("CF", [128, 272], F32)
    G = sb("G", [128, 4, NCH], F32)
    FG = sb("FG", [128, NCH], F32)
    ASC = sb("ASC", [128, NCH], F32)
    BDW = sb("BDW", [128, NCH], F32)
    LNG = sb("LNG", [128, NCH], F32)
    LNB = sb("LNB", [128, NCH], F32)
    WDW = sb("WDW", [128, NCH, 31], F32)
    QG = sb("QG", [128, 2], F32)
    WSTT = sb("WSTT", [128, 4, 128], BF16)
    RS = sb("RS", [128, 512], F32)
    SCRW = 14500
    SCR = sb("SCR", [128, SCRW], F32)
    PS = st.enter_context(nc.psum_tensor("PS", [128, 4096], F32))

    IDB = CB[:, 0:128]
    ONESM = CB[:, 128:256]
    ONESD = CB[:, 256:384]
    NTRI = CB[:, 384:512]
    NONES = CB[:, 512:640]
    MASK = [CB[:, 640 + jm * 512: 640 + (jm + 1) * 512] for jm in range(4)]
    IDF = CF[:, 0:128]
    TRIU = CF[:, 128:256]
    INVC = CF[:, 256:271]

    bX = [B("X%d" % c) for c in range(NCH)]
    bHB = [B("HB%d" % c) for c in range(NCH)]
    bM = [B("M%d" % c) for c in range(NCH)]
    bWBLK = [B("WBLK%d" % i) for i in range(NRING)]
    bC = B("consts")
    bRS = B("RS")
    bP = [B("P%d" % i) for i in range(8)]

    def bank(i, w=512):
        return PS[:, i * 512: i * 512 + w]

    def mm(out, lhsT, rhs, start, stop, reads, writes):
        S.op("pe", lambda h: h.matmul(out, lhsT=lhsT, rhs=rhs, start=start, stop=stop), reads, writes)

    def tr(out, in_, ident, reads, writes):
        S.op("pe", lambda h: h.transpose(out, in_, ident), reads, writes)

    def act(out, in_, func, reads, writes, bias=None, scale=None, accum=None):
        kw = {}
        if bias is not None:
            kw["bias"] = bias
        if scale is not None:
            kw["scale"] = scale
        if accum is not None:
            kw["accum_out"] = accum
        S.op("act", lambda h: h.activation(out=out, in_=in_, func=func, **kw), reads, writes)

    def tt(eng, out, in0, in1, op, reads, writes):
        S.op(eng, lambda h: h.tensor_tensor(out=out, in0=in0, in1=in1, op=op), reads, writes)

    def stt(eng, out, in0, scalar, in1, op0, op1, reads, writes):
        S.op(eng, lambda h: h.scalar_tensor_tensor(out=out, in0=in0, scalar=scalar, in1=in1, op0=op0, op1=op1), reads, writes)

    def ts(eng, out, in0, s1, op0, reads, writes, s2=None, op1=None):
        if op1 is None:
            S.op(eng, lambda h: h.tensor_scalar(out=out, in0=in0, scalar1=s1, scalar2=None, op0=op0), reads, writes)
        else:
            S.op(eng, lambda h: h.tensor_scalar(out=out, in0=in0, scalar1=s1, scalar2=s2, op0=op0, op1=op1), reads, writes)

    def cp(eng, out, in_, reads, writes):
        if eng == "act":
            S.op("act", lambda h: h.copy(out=out, in_=in_), reads, writes)
        else:
            S.op(eng, lambda h: h.tensor_copy(out=out, in_=in_), reads, writes)

    def ms(eng, out, val, writes):
        S.op(eng, lambda h: h.memset(out, val), (), writes)

    def dma(out, in_, reads=(), writes=(), is_output=False, slow=False):
        kw = {}
        if slow:
            kw["allow_slow_non_contiguous"] = True
        S.dma(out, in_, reads=reads, writes=writes, is_output=is_output, **kw)

    class Carver:
        def __init__(self):
            self.off = 0

        def f32(self, shape):
            n = int(np.prod(shape[1:]))
            assert self.off + n <= SCRW, ("scratch overflow", self.off, n)
            ap = SCR[:, self.off:self.off + n]
            self.off += n
            if len(shape) == 3:
                ap = ap.rearrange("p (a b) -> p a b", a=shape[1])
            return ap

        def bf16(self, shape):
            n = int(np.prod(shape[1:]))
            nw = (n + 1) // 2
            assert self.off + nw <= SCRW, ("scratch overflow", self.off, nw)
            ap = SCR[:, self.off:self.off + nw].bitcast(BF16)[:, 0:n]
            self.off += nw
            if len(shape) == 3:
                ap = ap.rearrange("p (a b) -> p a b", a=shape[1])
            return ap

    def col_load(dst, src1d):
        dma(dst, src1d.rearrange("(c p) -> p c", p=128), writes=[bC], slow=True)

    dma(CB[:, :], I["cb"][:, :], writes=[bC])
    dma(CF[:, :], I["cf"][:, :], writes=[bC])
    for i in range(4):
        col_load(G[:, i, :], I["norm_g"][i])
    col_load(FG[:, :], I["final_g"])
    col_load(ASC[:, :], I["a_scale"])
    col_load(BDW[:, :], I["b_b_dw"])
    col_load(LNG[:, :], I["b_ln_g"])
    col_load(LNB[:, :], I["b_ln_b"])
    for c in range(NCH):
        dma(WDW[:, c, :], I["b_w_dw"][:, c * 128:(c + 1) * 128].rearrange("k p -> p k"), writes=[bC], slow=True)
    dma(QG[:, 0:1], I["c_q_g"].rearrange("(p o) -> p o", o=1), writes=[bC], slow=True)
    dma(QG[:, 1:2], I["c_k_g"].rearrange("(p o) -> p o", o=1), writes=[bC], slow=True)
    ts("dve", QG[:, 0:1], QG[:, 0:1], float(128 ** -0.5), ALU.mult, [bC], [bC])
    cv0 = Carver()
    tmps = cv0.f32([128, 4, 128])
    tmps2 = cv0.f32([128, 4, 128])
    bT2 = B("tmps")
    bT3 = B("tmps2")
    dma(tmps, I["d_w_s"].rearrange("g t s -> t g s"), writes=[bT2])
    for g in range(4):
        tr(PS[:, g * 128:(g + 1) * 128], tmps[:, g, :], IDF, [bT2, bC], [bP[0]])
    cp("act", tmps2.rearrange("p a b -> p (a b)"), bank(0), [bP[0]], [bT3])
    tt("dve", WSTT[:, :, :], tmps2, TRIU.unsqueeze(1).broadcast_to([128, 4, 128]), ALU.mult, [bT3, bC], [bC])
    S.barrier()

    ORD = {}
    o = []
    for g in range(4):
        o += [2 * g, 2 * g + 1, 8 + 2 * g, 8 + 2 * g + 1]
    ORD["a_in"] = o
    o = []
    for c in range(NCH):
        o += [8 + c, c]
    ORD["b_in"] = o + [16 + c for c in range(NCH)]
    o = []
    for h in range(8):
        o += [h, 8 + h, 16 + h, 24 + h]
    ORD["c_in"] = o
    o = [8 + q for q in range(8)]
    for c in range(NCH):
        o += [c, 16 + c]
    ORD["d_in"] = o
    OUTO = list(range(8))
    MATS = [("a_in", I["a_w_in"], 2048, 0, ORD["a_in"]), ("a_out", I["a_w_out"], 1024, None, OUTO),
            ("b_in", I["b_w_in"], 3072, 1, ORD["b_in"]), ("b_out", I["b_w_out"], 1024, None, OUTO),
            ("c_in", I["c_w_in"], 4096, 2, ORD["c_in"]), ("c_out", I["c_w_out"], 1024, None, OUTO),
            ("d_in", I["d_w_in"], 3072, 3, ORD["d_in"]), ("d_out", I["d_w_out"], 1024, None, OUTO)]
    BASE = {}
    nb = 0
    for (nm, _, F_, _, _) in MATS:
        BASE[nm] = nb
        nb += F_ // 256
    NBLK = nb
    WSC = nc.dram_tensor("wsc", [NBLK, 128, 2048], BF16, kind="Internal").ap()
    bWSC = B("WSC")

    def prepass():
        if TMAX >= 2048:
            ASM0 = X[:, :, :].rearrange("p c t -> p (c t)").bitcast(BF16)
            ASM1 = HB[:, :, :].rearrange("p c t -> p (c t)")
        else:
            ASM0 = sb("ASM0", [128, 32768], BF16)[:, :]
            ASM1 = sb("ASM1", [128, 8192], BF16)[:, :]
        bA = [B("ASM0"), B("ASM1")]
        STG = [SCR[:, 0:4096].rearrange("p (c f) -> p c f", c=NCH), SCR[:, 4096:8192].rearrange("p (c f) -> p c f", c=NCH)]
        bSTG = [B("STG0"), B("STG1")]
        gi = 0
        ei = 0
        for (nm, w_ap, F_, gl, order) in MATS:
            nblk = F_ // 256
            which = 0 if nblk > 4 else 1
            A_ = ASM0 if which == 0 else ASM1
            asm = A_[:, 0:nblk * 2048].rearrange("p (b c f) -> p b c f", c=NCH, f=256)
            for j in range(F_ // 512):
                sl = gi % 2
                gi += 1
                dma(STG[sl], w_ap[:, j * 512:(j + 1) * 512].rearrange("(c p) f -> p c f", p=128), writes=[bSTG[sl]])
                for q in range(4):
                    pos = order.index(4 * j + q)
                    bb, half = pos // 2, pos % 2
                    src = STG[sl][:, :, q * 128:(q + 1) * 128]
                    dst = asm[:, bb, :, half * 128:(half + 1) * 128]
                    eng = "dve" if ei % 2 == 0 else "pool"
                    ei += 1
                    if gl is None:
                        cp(eng, dst, src, [bSTG[sl]], [bA[which]])
                    else:
                        tt(eng, dst, src, G[:, gl, :].unsqueeze(2).broadcast_to([128, NCH, 128]), ALU.mult,
                           [bSTG[sl], bC], [bA[which]])
            dma(WSC[BASE[nm]:BASE[nm] + nblk].rearrange("b p x -> p b x"),
                A_[:, 0:nblk * 2048].rearrange("p (b x) -> p b x", x=2048), reads=[bA[which]], writes=[bWSC])
        S.barrier()

    prepass()

    PLAN = []
    gstate = dict(issued=0)

    def fetch_upto(k):
        k = min(k, len(PLAN) - 1)
        while gstate["issued"] <= k:
            i = gstate["issued"]
            sl = i % NRING
            dma(WBLK[sl][:, :, :].rearrange("p c f -> p (c f)"), WSC[PLAN[i]], reads=[bWSC], writes=[bWBLK[sl]])
            gstate["issued"] += 1

    class WStream:
        def __init__(self, name, first_chunk=0):
            self.pbase = len(PLAN)
            self.first = first_chunk

        def get(self, i):
            idx = self.pbase + i // 2
            fetch_upto(idx + 2)
            sl = idx % NRING
            half = i % 2
            return WBLK[sl][:, :, half * 128:(half + 1) * 128], bWBLK[sl]

    def plan_stream(name, nchunks, first_chunk=0):
        ws = WStream(name)
        b0 = BASE[name] + first_chunk // 2
        for k in range(nchunks // 2):
            PLAN.append(b0 + k)
        return ws

    mmrot = [0]

    mmbanks = [[0, 1]]

    def next_mm_bank():
        lst = mmbanks[0]
        b = lst[mmrot[0] % len(lst)]
        mmrot[0] += 1
        return b

    def segment(kind, s):
        T = TP if kind == "p" else TS
        xin = I["xp"] if kind == "p" else I["xs"]
        n = min(512, T)
        TT = [(t0, n) for t0 in range(0, T, n)]
        ns = min(128, T)
        SUB = [(s0, ns) for s0 in range(0, T, ns)]
        NSUB = len(SUB)
        start_pos = 0 if kind == "p" else PAST

        def inproj(wb, bwb, t0, bk):
            for c in range(NCH):
                mm(bank(bk, n), wb[:, c, :], HB[:, c, t0:t0 + n], c == 0, c == NCH - 1, [bwb] + bHB, [bP[bk]])

        def rstd_from(bk, width, out, bout, scale=1.0):
            act(out, bank(bk, width), ACTF.Ln, [bP[bk]], [bout], bias=EPS, scale=scale)
            act(out, out, ACTF.Exp, [bout], [bout], scale=-0.5)

        cv = Carver()
        IST = [cv.f32([128, D]) for _ in range(2)]
        bIST = [B("IST0"), B("IST1")]
        for j, (s0, _) in enumerate(SUB):
            sl = j % 2
            dma(IST[sl][:ns, :], xin[s, s0:s0 + ns, :], writes=[bIST[sl]])
            for c in range(NCH):
                tr(PS[:, 1024 + c * 128: 1024 + c * 128 + ns], IST[sl][:ns, c * 128:(c + 1) * 128], IDF[:ns, :ns],
                   [bIST[sl], bC], [bP[2], bP[3]])
            src = PS[:, 1024:2048].rearrange("p (c t) -> p c t", c=NCH)[:, :, 0:ns]
            cp("act" if j % 2 == 0 else "dve", X[:, :, s0:s0 + ns], src, [bP[2], bP[3]], bX)
        S.barrier()

        def norm_in(li):
            cvn = Carver()
            SQ2 = [cvn.bf16([128, NCH, 512]) for _ in range(2)]
            RS2 = [cvn.f32([128, 512]) for _ in range(2)]
            bSQ2 = [B("SQ0n"), B("SQ1n")]
            bRS2 = [B("RS0n"), B("RS1n")]

            def stage1(i):
                t0 = TT[i][0]
                sl = i % 2
                for c in range(NCH):
                    act(SQ2[sl][:, c, :n], X[:, c, t0:t0 + n], ACTF.Square, [bX[c]], [bSQ2[sl]])
                for c in range(NCH):
                    mm(bank(4 + sl, n), ONESM, SQ2[sl][:, c, :n], c == 0, c == NCH - 1, [bC, bSQ2[sl]], [bP[4 + sl]])
                rstd_from(4 + sl, n, RS2[sl][:, :n], bRS2[sl])

            def stage2(i):
                t0 = TT[i][0]
                sl = i % 2
                for c in range(NCH):
                    tt("pool" if c % 4 == 3 else "dve", HB[:, c, t0:t0 + n], X[:, c, t0:t0 + n], RS2[sl][:, :n], ALU.mult,
                       [bX[c], bRS2[sl]], [bHB[c]])

            for i in range(len(TT) + 1):
                if i < len(TT):
                    stage1(i)
                if i >= 1:
                    stage2(i - 1)

        def outproj(w_out, mfun, bmfun):
            ws = plan_stream(w_out, 8)
            for fo in range(NCH):
                wb, bwb = ws.get(fo)
                for (t0, _) in TT:
                    bk = next_mm_bank()
                    for c in range(NCH):
                        mm(bank(bk, n), wb[:, c, :], mfun(c, t0), c == 0, c == NCH - 1, [bwb, bmfun(c)], [bP[bk]])
                    tt("dve", X[:, fo, t0:t0 + n], X[:, fo, t0:t0 + n], bank(bk, n), ALU.add, [bX[fo], bP[bk]], [bX[fo]])

        def layer0():
            norm_in(0)
            S.barrier()
            cvl = Carver()
            mmbanks[0] = [0, 1, 2, 3, 6, 7]
            UFS = [cvl.f32([128, 15 + T]) for _ in range(2)]
            TA = cvl.f32([128, 15 + T])
            TB = cvl.f32([128, 15 + T])
            PL = [cvl.bf16([128, T]) for _ in range(2)]
            SG = [cvl.bf16([128, T]) for _ in range(2)]
            bUFS = [B("UF0"), B("UF1")]
            bTA, bTB = B("TA"), B("TB")
            bPL = [B("PL0"), B("PL1")]
            bSG = [B("SG0"), B("SG1")]
            WGRP = cvl.bf16([128, 8, 256])
            bWG = B("WGRP")
            POUT = cvl.f32([128, D])
            bPOUT = B("POUT")
            if kind == "s":
                CPT = cvl.f32([128, D])
                bCPT = B("CPT")
                dma(CPT[:15, :], I["cpool"][s, :, :], writes=[bCPT])
            tmpw = (TA[:, 0:2048] if 15 + T >= 2048 else cvl.f32([128, 2048])).rearrange("p (a b) -> p a b", a=8)
            dma(tmpw, I["a_w_grp"].rearrange("g (kc p) d -> p (g kc) d", p=128), writes=[bTA])
            cp("pool", WGRP[:, :, :], tmpw, [bTA], [bWG])
            w_in = I["a_w_in"]
            order = []
            for g in range(4):
                order += [2 * g, 2 * g + 1, 8 + 2 * g, 8 + 2 * g + 1]
            assert order == ORD["a_in"]
            ws = plan_stream("a_in", 16)
            wi = 0
            for g in range(4):
                w = 2 ** (g + 1)
                for oc in range(2):
                    c = 2 * g + oc
                    UF, bUF = UFS[oc], bUFS[oc]
                    wb, bwb = ws.get(wi)
                    wi += 1
                    if kind == "p":
                        ms("pool", UF[:, 0:15], 0.0, [bUF])
                    else:
                        tr(bank(4, 15), CPT[:15, c * 128:(c + 1) * 128], IDF[:15, :15], [bCPT, bC], [bP[4]])
                        cp("act", UF[:, 0:15], bank(4, 15), [bP[4]], [bUF])
                    for (t0, _) in TT:
                        bk = next_mm_bank()
                        inproj(wb, bwb, t0, bk)
                        cp("act", UF[:, 15 + t0:15 + t0 + n], bank(bk, n), [bP[bk]], [bUF])
                    tr(PS[:15, 5 * 512:5 * 512 + 128], UF[:, T:T + 15], IDF, [bUF, bC], [bP[5]])
                    cp("act", POUT[:15, c * 128:(c + 1) * 128], PS[:15, 5 * 512:5 * 512 + 128], [bP[5]], [bPOUT])
                    W = 15 + T
                    src, bsrc = UF, bUF
                    pp = [(TA, bTA), (TB, bTB)]
                    for k in range(g + 1):
                        sh = 2 ** k
                        lo = 2 ** (k + 1) - 1
                        dstt, bdst = pp[k % 2]
                        tt("pool" if k % 2 == 0 else "dve", dstt[:, lo:W], src[:, lo:W], src[:, lo - sh:W - sh], ALU.add,
                           [bsrc], [bdst])
                        src, bsrc = dstt, bdst
                    stt("dve", PL[oc][:, 0:T], src[:, 15:15 + T], float(1.0 / w), UF[:, 15:15 + T], ALU.mult, ALU.subtract,
                        [bsrc, bUF], [bPL[oc]])
                    if start_pos == 0 and w > 1:
                        other, bother = pp[(g + 1) % 2]
                        m1 = min(w - 1, T)
                        tt("dve", other[:, 0:m1], src[:, 15:15 + m1], INVC[:, 0:m1], ALU.mult, [bsrc, bC], [bother])
                        tt("dve", PL[oc][:, 0:m1], other[:, 0:m1], UF[:, 15:15 + m1], ALU.subtract, [bother, bUF], [bPL[oc]])
                for oc in range(2):
                    wb, bwb = ws.get(wi)
                    wi += 1
                    for (t0, _) in TT:
                        bk = next_mm_bank()
                        inproj(wb, bwb, t0, bk)
                        act(SG[oc][:, t0:t0 + n], bank(bk, n), ACTF.Silu, [bP[bk]], [bSG[oc]])
                for oc in range(2):
                    c = 2 * g + oc
                    for (t0, _) in TT:
                        bk = next_mm_bank()
                        for kc in range(2):
                            mm(bank(bk, n), WGRP[:, g * 2 + kc, oc * 128:(oc + 1) * 128], PL[kc][:, t0:t0 + n], kc == 0, kc == 1,
                               [bWG, bPL[kc]], [bP[bk]])
                        stt("dve", MB[:, c, t0:t0 + n], bank(bk, n), ASC[:, c:c + 1], SG[oc][:, t0:t0 + n], ALU.mult, ALU.mult,
                            [bP[bk], bC, bSG[oc]], [bM[c]])
            dma((O["poolp"] if kind == "p" else O["pools"])[s, :, :], POUT[:15, :], reads=[bPOUT], is_output=True)
            outproj("a_out", lambda c, t0: MB[:, c, t0:t0 + n], lambda c: bM[c])
            S.barrier()

        def layer1():
            norm_in(1)
            S.barrier()
            cvl = Carver()
            HF = cvl.bf16([128, NCH, 30 + T])
            CO = cvl.f32([128, D])
            HSTC = cvl.f32([128, 32])
            SB_ = [cvl.f32([128, 512]) for _ in range(2)]
            MU = cvl.f32([128, 512])
            VAR = cvl.f32([128, 512])
            RSD = cvl.f32([128, 512])
            YB = [cvl.bf16([128, 512]) for _ in range(2)]
            YQ = [cvl.bf16([128, 512]) for _ in range(2)]
            TG = [cvl.bf16([128, 512]) for _ in range(2)]
            MBflat = MB[:, :, :].rearrange("p c t -> p (c t)")
            nY = NCH * 512
            assert 2 * nY + 2 * 31 * 128 <= NCH * TMAX or True
            if NCH * TMAX >= 2 * nY + 2 * 31 * 128:
                Y = MBflat[:, 0:2 * nY].bitcast(F32).rearrange("p (c t) -> p c t", c=NCH)
                DG = [MBflat[:, 2 * nY + i * 31 * 128: 2 * nY + (i + 1) * 31 * 128].rearrange("p (k f) -> p k f", k=31)
                      for i in range(2)]
            else:
                Y = cvl.f32([128, NCH, 512])
                DG = [cvl.bf16([128, 31, 128]) for _ in range(2)]
            bHF = [B("HF%d" % c) for c in range(NCH)]
            bCO, bHST = B("CO"), B("HSTC")
            bSB = [B("SB0"), B("SB1")]
            bMU, bVAR, bRSD = B("MU"), B("VAR"), B("RSD")
            bYB = [B("YB0"), B("YB1")]
            bYQ = [B("YQ0"), B("YQ1")]
            bTG = [B("TG0"), B("TG1")]
            bY = [B("Y%d" % c) for c in range(NCH)]
            bDG = [B("DG0"), B("DG1")]
            w_in = I["b_w_in"]
            order = []
            for c in range(NCH):
                order += [8 + c, c]
            order += [16 + c for c in range(NCH)]
            assert order == ORD["b_in"]
            ws = plan_stream("b_in", 24)
            if kind == "p":
                for c in range(NCH):
                    ms("pool", HF[:, c, 0:30], 0.0, [bHF[c]])
            else:
                dma(CO[:30, :], I["cconv"][s, :, :], writes=[bCO])
                for c in range(NCH):
                    tr(bank(4, 30), CO[:30, c * 128:(c + 1) * 128], IDF[:30, :30], [bCO, bC], [bP[4]])
                    cp("act", HF[:, c, 0:30], bank(4, 30), [bP[4]], [bHF[c]])
            mmbanks[0] = [0, 1, 2, 3]
            wi = 0
            m1 = min(30, T)
            sbi = 0
            for c in range(NCH):
                wbb, bwbb = ws.get(wi)
                wa, bwa = ws.get(wi + 1)
                wi += 2
                for ti, (t0, _) in enumerate(TT):
                    sl = sbi % 2
                    sbi += 1
                    bkb = next_mm_bank()
                    inproj(wbb, bwbb, t0, bkb)
                    act(SB_[sl][:, :n], bank(bkb, n), ACTF.Sigmoid, [bP[bkb]], [bSB[sl]])
                    bk = next_mm_bank()
                    inproj(wa, bwa, t0, bk)
                    tt("dve", HF[:, c, 30 + t0:30 + t0 + n], bank(bk, n), SB_[sl][:, :n], ALU.mult, [bP[bk], bSB[sl]], [bHF[c]])
                    if ti == len(TT) - 1:
                        tt("dve", HSTC[:, 0:m1], bank(bk, n)[:, n - m1:n], SB_[sl][:, n - m1:n], ALU.mult,
                           [bP[bk], bSB[sl]], [bHST])
                        tr(PS[:m1, 5 * 512:5 * 512 + 128], HSTC[:, 0:m1], IDF, [bHST, bC], [bP[5]])
                        cp("act", CO[:m1, c * 128:(c + 1) * 128], PS[:m1, 5 * 512:5 * 512 + 128], [bP[5]], [bCO])
            cdst = O["convp"] if kind == "p" else O["convs"]
            dma(cdst[s, 30 - m1:30, :], CO[:m1, :], reads=[bCO], is_output=True)
            if m1 < 30:
                dma(cdst[s, 0:30 - m1, :], I["cconv"][s, m1:30, :], reads=[bCO], is_output=True)
            dgi = 0
            for (t0, _) in TT:
                for c in range(NCH):
                    sl = dgi % 2
                    dgi += 1
                    tt("dve", DG[sl][:, :, :], IDB.unsqueeze(1).broadcast_to([128, 31, 128]),
                       WDW[:, c, :].unsqueeze(2).broadcast_to([128, 31, 128]), ALU.mult, [bC], [bDG[sl]])
                    bk = 6 + (c % 2)
                    for k in range(31):
                        mm(bank(bk, n), DG[sl][:, k, :], HF[:, c, t0 + k:t0 + k + n], k == 0, k == 30, [bDG[sl], bHF[c]], [bP[bk]])
                    act(Y[:, c, :n], bank(bk, n), ACTF.Identity, [bP[bk]], [bY[c]], bias=BDW[:, c:c + 1])
                    ysl = c % 2
                    cp("pool", YB[ysl][:, :n], Y[:, c, :n], [bY[c]], [bYB[ysl]])
                    act(YQ[ysl][:, :n], Y[:, c, :n], ACTF.Square, [bY[c]], [bYQ[ysl]])
                    if c >= 1:
                        pc = c - 1
                        mm(bank(4, n), ONESM, YB[pc % 2][:, :n], pc == 0, False, [bC, bYB[pc % 2]], [bP[4]])
                        mm(bank(5, n), ONESM, YQ[pc % 2][:, :n], pc == 0, False, [bC, bYQ[pc % 2]], [bP[5]])
                pc = NCH - 1
                mm(bank(4, n), ONESM, YB[pc % 2][:, :n], False, True, [bC, bYB[pc % 2]], [bP[4]])
                mm(bank(5, n), ONESM, YQ[pc % 2][:, :n], False, True, [bC, bYQ[pc % 2]], [bP[5]])
                cp("act", MU[:, :n], bank(4, n), [bP[4]], [bMU])
                tt("dve", VAR[:, :n], MU[:, :n], MU[:, :n], ALU.mult, [bMU], [bVAR])
                tt("dve", VAR[:, :n], bank(5, n), VAR[:, :n], ALU.subtract, [bP[5], bVAR], [bVAR])
                act(RSD[:, :n], VAR[:, :n], ACTF.Ln, [bVAR], [bRSD], bias=EPS)
                act(RSD[:, :n], RSD[:, :n], ACTF.Exp, [bRSD], [bRSD], scale=-0.5)
                for c in range(NCH):
                    e1 = "dve" if c % 2 == 0 else "pool"
                    tt(e1, Y[:, c, :n], Y[:, c, :n], MU[:, :n], ALU.subtract, [bY[c], bMU], [bY[c]])
                    tt(e1, Y[:, c, :n], Y[:, c, :n], RSD[:, :n], ALU.mult, [bY[c], bRSD], [bY[c]])
                    act(HF[:, c, t0:t0 + n], Y[:, c, :n], ACTF.Silu, [bY[c], bC], [bHF[c]], bias=LNB[:, c:c + 1], scale=LNG[:, c:c + 1])
            for c in range(NCH):
                wg, bwg = ws.get(wi)
                wi += 1
                for ti, (t0, _) in enumerate(TT):
                    bk = next_mm_bank()
                    inproj(wg, bwg, t0, bk)
                    sl = ti % 2
                    act(TG[sl][:, :n], bank(bk, n), ACTF.Silu, [bP[bk]], [bTG[sl]])
                    tt("dve" if ti % 2 == 0 else "pool", HF[:, c, t0:t0 + n], HF[:, c, t0:t0 + n], TG[sl][:, :n], ALU.mult,
                       [bHF[c], bTG[sl]], [bHF[c]])
            outproj("b_out", lambda c, t0: HF[:, c, t0:t0 + n], lambda c: bHF[c])
            S.barrier()

        def layer2():
            norm_in(2)
            S.barrier()
            cvl = Carver()
            KN = cvl.bf16([128, T])
            SGH = cvl.bf16([128, T])
            VH = cvl.bf16([128, NSUB, 128])
            STG = [cvl.f32([128, 4, 128]) for _ in range(2)]
            QF = [cvl.f32([128, n]) for _ in range(2)]
            SQ = [cvl.bf16([128, n]) for _ in range(2)]
            RQ = [cvl.f32([128, n]) for _ in range(2)]
            KFN = [cvl.f32([128, n]) for _ in range(2)]
            E32 = [cvl.f32([128, 2 * n]) for _ in range(3)]
            SPB = [cvl.bf16([128, 2 * n]) for _ in range(3)]
            CSS = cvl.f32([128, n])
            AB = [cvl.bf16([128, 2 * n]) for _ in range(2)]
            if kind == "s":
                NPT = PAST // 128
                KC = cvl.f32([128, NPT, 128])
                VC = cvl.f32([128, NPT, 128])
                KPT = [cvl.bf16([128, 128]) for _ in range(8)]
                VPB = [cvl.bf16([128, 128]) for _ in range(8)]
                bKC, bVC = B("KC"), B("VC")
                bKPT = [B("KPT%d" % i) for i in range(8)]
                bVPB = [B("VPB%d" % i) for i in range(8)]
            bKN, bSGH, bVH = B("KN"), B("SGH"), B("VH")
            bSTG = [B("STG0"), B("STG1")]
            bQF = [B("QF0"), B("QF1")]
            bCSS = B("CSS")
            bSQ = [B("SQ0"), B("SQ1")]
            bRQ = [B("RQ0"), B("RQ1")]
            bKFN = [B("KFN0"), B("KFN1")]
            bE = [B("E0"), B("E1"), B("E2")]
            bSP = [B("SP0"), B("SP1"), B("SP2")]
            bAB = [B("AB0"), B("AB1")]
            w_in = I["c_w_in"]
            order = []
            for h in range(8):
                order += [h, 8 + h, 16 + h, 24 + h]
            assert order == ORD["c_in"]
            ws = plan_stream("c_in", 32)
            ko = O["kp"] if kind == "p" else O["ks"]
            vo = O["vp"] if kind == "p" else O["vs"]
            wi = 0
            qfi = [0]
            stepi = [0]

            stg_ctr = [0]

            def tm_out(src_f32, bsrc, ti, dst, h, to_vh):
                sg = stg_ctr[0] % 2
                stg_ctr[0] += 1
                pb = 6 + sg
                nsub = (n + ns - 1) // ns
                for j in range(nsub):
                    tr(PS[:ns, pb * 512 + j * 128: pb * 512 + (j + 1) * 128], src_f32[:, j * ns:(j + 1) * ns], IDF, [bsrc, bC], [bP[pb]])
                srcp = PS[:ns, pb * 512:pb * 512 + nsub * 128].rearrange("p (j d) -> p j d", j=nsub)
                cp("act", STG[sg][:ns, 0:nsub, :], srcp, [bP[pb]], [bSTG[sg]])
                j0 = ti * 4
                if T >= 128:
                    dma(dst[s].rearrange("(j p) (hh d) -> p j hh d", p=128, hh=8)[:, j0:j0 + nsub, h, :], STG[sg][:, 0:nsub, :],
                        reads=[bSTG[sg]], is_output=True)
                else:
                    dma(dst[s, :, h * 128:(h + 1) * 128], STG[sg][:ns, 0, :], reads=[bSTG[sg]], is_output=True)
                if to_vh:
                    cp("pool", VH[:ns, j0:j0 + nsub, :], STG[sg][:ns, 0:nsub, :], [bSTG[sg]], [bVH])

            def views(i, NT, nk0):
                e = i % 3
                zb2 = e * 2
                if NT == 2:
                    zall = PS[:nk0, zb2 * 512:(zb2 + 2) * 512].rearrange("p (t q) -> p t q", t=2)[:, :, 0:n]
                    eall = E32[e][:nk0, 0:2 * n].rearrange("p (t q) -> p t q", t=2)
                    sall = SPB[e][:nk0, 0:2 * n].rearrange("p (t q) -> p t q", t=2)
                    aall = AB[i % 2][:nk0, 0:2 * n].rearrange("p (t q) -> p t q", t=2)
                else:
                    zall = bank(zb2, n)[:nk0, :]
                    eall = E32[e][:nk0, 0:n]
                    sall = SPB[e][:nk0, 0:n]
                    aall = AB[i % 2][:nk0, 0:n]
                return e, zb2, zall, eall, sall, aall

            def sstep_A1(i, tiles, q_ap, bq):
                NT = len(tiles)
                nk0 = tiles[0][2]
                e, zb2, zall, eall, sall, aall = views(i, NT, nk0)
                bz = [bP[zb2 + t_] for t_ in range(NT)]
                for t_, (kT, bkT, nk, mask) in enumerate(tiles):
                    zt = bank(zb2 + t_, n)[:nk, :]
                    mm(zt, kT, q_ap, True, mask is None, [bkT, bq], [bP[zb2 + t_]])
                    if mask is not None:
                        mm(zt, IDB[:nk, :nk], mask, False, True, [bC], [bP[zb2 + t_]])
                act(eall, zall, ACTF.Exp, bz, [bE[e]])
                act(sall, eall, ACTF.Ln, [bE[e]], [bSP[e]], bias=1.0)

            def sstep_A2(i, tiles):
                NT = len(tiles)
                nk0 = tiles[0][2]
                e, zb2, zall, eall, sall, aall = views(i, NT, nk0)
                for t_, (kT, bkT, nk, mask) in enumerate(tiles):
                    mm(bank(zb2 + t_, n)[:nk, :], NTRI[:nk, :nk], SPB[e][:nk, t_ * n:(t_ + 1) * n], False, True, [bC, bSP[e]], [bP[zb2 + t_]])
                if NT == 2:
                    mm(bank(zb2 + 1, n)[:nk0, :], NONES[:nk0, :nk0], SPB[e][:nk0, 0:n], False, True, [bC, bSP[e]], [bP[zb2 + 1]])

            def sstep_B0(i, vtiles, first, last):
                NT = len(vtiles)
                nk0 = vtiles[0][2]
                e = i % 3
                if not last:
                    for t_ in range(NT):
                        mm(bank(6, n), NONES[:nk0, :], SPB[e][:nk0, t_ * n:(t_ + 1) * n], t_ == 0, t_ == NT - 1, [bC, bSP[e]], [bP[6]])

            def sstep_B(i, vtiles, first, last):
                NT = len(vtiles)
                nk0 = vtiles[0][2]
                e, zb2, zall, eall, sall, aall = views(i, NT, nk0)
                a2 = i % 2
                bz = [bP[zb2 + t_] for t_ in range(NT)]
                if first:
                    act(aall, zall, ACTF.Exp, bz, [bAB[a2]])
                    if not last:
                        cp("dve", CSS[:, :n], bank(6, n), [bP[6]], [bCSS])
                else:
                    if NT == 2:
                        tt("dve", eall, zall, CSS[:nk0, :n].unsqueeze(1).broadcast_to([nk0, 2, n]), ALU.add, bz + [bCSS], [bE[e]])
                    else:
                        tt("dve", eall, zall, CSS[:nk0, :n], ALU.add, bz + [bCSS], [bE[e]])
                    act(aall, eall, ACTF.Exp, [bE[e]], [bAB[a2]])
                    if not last:
                        tt("dve", CSS[:, :n], CSS[:, :n], bank(6, n), ALU.add, [bCSS, bP[6]], [bCSS])

            def sstep_C(i, vtiles, first, last):
                NT = len(vtiles)
                a2 = i % 2
                for t_, (vT, bvT, nk) in enumerate(vtiles):
                    mm(bank(7, n), vT, AB[a2][:nk, t_ * n:(t_ + 1) * n], first and t_ == 0, last and t_ == NT - 1, [bvT, bAB[a2]], [bP[7]])

            for h in range(8):
                if kind == "s":
                    dma(KC[:, :, :], I["ck"][s, :, h * 128:(h + 1) * 128].rearrange("(j p) d -> p j d", p=128), writes=[bKC])
                    dma(VC[:, :, :], I["cv"][s, :, h * 128:(h + 1) * 128].rearrange("(j p) d -> p j d", p=128), writes=[bVC])
                items = [(kd, ti) for kd in ("q", "k", "v", "g") for ti in range(len(TT))]
                NI = len(items)
                wcache = {}

                def S1(j):
                    kd, ti = items[j]
                    t0 = TT[ti][0]
                    if kd not in wcache:
                        wcache[kd] = ws.get(wi + "qkvg".index(kd))
                    wb_, bwb_ = wcache[kd]
                    bk = j % 6
                    inproj(wb_, bwb_, t0, bk)
                    if kd in ("q", "k"):
                        act(SQ[j % 2][:, :n], bank(bk, n), ACTF.Square, [bP[bk]], [bSQ[j % 2]])
                    elif kd == "v":
                        cp("act", QF[j % 2][:, :n], bank(bk, n), [bP[bk]], [bQF[j % 2]])
                    else:
                        act(SGH[:, t0:t0 + n], bank(bk, n), ACTF.Silu, [bP[bk]], [bSGH])

                def S2(j):
                    kd, ti = items[j]
                    t0 = TT[ti][0]
                    bk = j % 6
                    if kd in ("q", "k"):
                        sb_ = 6 + (j % 2)
                        mm(bank(sb_, n), ONESD, SQ[j % 2][:, :n], True, True, [bC, bSQ[j % 2]], [bP[sb_]])
                        rstd_from(sb_, n, RQ[j % 2][:, :n], bRQ[j % 2])
                        if kd == "q":
                            stt("dve", MB[:, h, t0:t0 + n], bank(bk, n), QG[:, 0:1], RQ[j % 2][:, :n], ALU.mult, ALU.mult,
                                [bP[bk], bC, bRQ[j % 2]], [bM[h]])
                        else:
                            stt("dve", KFN[j % 2][:, :n], bank(bk, n), QG[:, 1:2], RQ[j % 2][:, :n], ALU.mult, ALU.mult,
                                [bP[bk], bC, bRQ[j % 2]], [bKFN[j % 2]])
                            cp("pool", KN[:, t0:t0 + n], KFN[j % 2][:, :n], [bKFN[j % 2]], [bKN])
                    elif kd == "v":
                        tm_out(QF[j % 2], bQF[j % 2], ti, vo, h, True)

                def S3(j):
                    kd, ti = items[j]
                    if kd == "k":
                        tm_out(KFN[j % 2], bKFN[j % 2], ti, ko, h, False)

                for j in range(NI + 2):
                    if j < NI:
                        S1(j)
                    if 0 <= j - 1 < NI:
                        S2(j - 1)
                    if 0 <= j - 2 < NI:
                        S3(j - 2)
                wi += 4
                for (t0, _) in TT:
                    own = [(k0, nk) for (k0, nk) in SUB if k0 < t0 + n][::-1]
                    steps = []
                    for (k0, nk) in own:
                        mask = None
                        if k0 >= t0:
                            mask = MASK[(k0 - t0) // 128][:nk, :n]
                        steps.append(("own", k0, nk, mask))
                    if kind == "s":
                        for pt in range(PAST // 128 - 1, -1, -1):
                            steps.append(("past", pt, 128, None))
                    q_ap = MB[:, h, t0:t0 + n]
                    pi = 0
                    ssteps = []
                    cur = []
                    for st_ in steps:
                        if st_[2] == 128 and n == 512 or (st_[0] == "past"):
                            cur.append(st_)
                            if len(cur) == 2:
                                ssteps.append(cur)
                                cur = []
                        else:
                            if cur:
                                ssteps.append(cur)
                                cur = []
                            ssteps.append([st_])
                    if cur:
                        ssteps.append(cur)
                    prepared = []

                    def prep(si):
                        grp = ssteps[si]
                        tiles = []
                        vtiles = []
                        for t_, (typ, k0, nk, mask) in enumerate(grp):
                            if typ == "own":
                                tiles.append((KN[:, k0:k0 + nk], bKN, nk, mask))
                                vtiles.append((VH[:nk, k0 // 128, :], bVH, nk))
                            else:
                                sl = (si % 4) * 2 + t_
                                tr(PS[:, 6 * 512 + 256:6 * 512 + 384], KC[:, k0, :], IDF, [bKC, bC], [bP[6]])
                                cp("dve", KPT[sl][:, :], PS[:, 6 * 512 + 256:6 * 512 + 384], [bP[6]], [bKPT[sl]])
                                cp("pool", VPB[sl][:, :], VC[:, k0, :], [bVC], [bVPB[sl]])
                                tiles.append((KPT[sl][:, :], bKPT[sl], 128, None))
                                vtiles.append((VPB[sl][:, :], bVPB[sl], 128))
                        return tiles, vtiles

                    NSS = len(ssteps)
                    TV = {}
                    for k in range(NSS + 3):
                        if 0 <= k - 2 < NSS:
                            sstep_B0(k - 2, TV[k - 2][1], k - 2 == 0, k - 2 == NSS - 1)
                        if k < NSS:
                            TV[k] = prep(k)
                            sstep_A1(k, TV[k][0], q_ap, bM[h])
                        if 0 <= k - 1 < NSS:
                            sstep_A2(k - 1, TV[k - 1][0])
                        if 0 <= k - 2 < NSS:
                            sstep_B(k - 2, TV[k - 2][1], k - 2 == 0, k - 2 == NSS - 1)
                        if 0 <= k - 3 < NSS:
                            sstep_C(k - 3, TV[k - 3][1], k - 3 == 0, k - 3 == NSS - 1)
                    tt("dve", MB[:, h, t0:t0 + n], bank(7, n), SGH[:, t0:t0 + n], ALU.mult, [bP[7], bSGH], [bM[h]])
            mmbanks[0] = [0, 1, 2, 3, 4, 5]
            outproj("c_out", lambda c, t0: MB[:, c, t0:t0 + n], lambda c: bM[c])
            S.barrier()

        def layer3():
            norm_in(3)
            S.barrier()
            cvl = Carver()
            VT = cvl.bf16([128, NSUB, D])
            BS = cvl.f32([128, 4, 128])
            DVG = cvl.f32([128, D])
            bL3C = B("L3C")
            dma(BS[:, :, :], I["d_b_s"].partition_broadcast(128), writes=[bL3C])
            dma(DVG[:, :], I["d_v_g"].partition_broadcast(128), writes=[bL3C])
            if TMAX >= D:
                WV = MB[:, :, 0:D]
            else:
                WV = cvl.bf16([128, NCH, D])
            SSQ = cvl.f32([128, 2])
            VF = cvl.f32([128, D]) if kind == "s" else None
            UC = [cvl.f32([128, 512]) for _ in range(2)]
            SGC = [cvl.f32([128, 512]) for _ in range(2)]
            JUNK = UC[0].bitcast(BF16)
            bVT, bWV, bVF, bSSQ = B("VT"), B("WV"), B("VF"), B("SSQ")
            bUC = [B("UC0"), B("UC1")]
            bJ = bUC[0]
            bSGC = [B("SGC0"), B("SGC1")]
            w_in = I["d_w_in"]
            b0 = BASE["d_in"]
            for bq in range(4):
                dma(WV[:, :, bq * 256:(bq + 1) * 256], WSC[b0 + bq].rearrange("p (c f) -> p c f", c=NCH), reads=[bWSC], writes=[bWV])
            SSQ2 = [SSQ, cvl.f32([128, 2])]
            bSSQ2 = [bSSQ, B("SSQb")]
            JUNK2 = [JUNK, cvl.bf16([128, D])]
            bJ2 = [bJ, B("JUNKb")]
            for j, (s0, _) in enumerate(SUB):
                r = j % 2
                rb = 2 if r == 0 else 6
                Sq, bSq = SSQ2[r], bSSQ2[r]
                for half in range(2):
                    for c in range(NCH):
                        mm(PS[:ns, (rb + half) * 512: (rb + half + 1) * 512], HB[:, c, s0:s0 + ns], WV[:, c, half * 512:(half + 1) * 512],
                           c == 0, c == NCH - 1, bHB + [bWV], [bP[rb + half]])
                pr = PS[:ns, rb * 512:rb * 512 + 1024]
                ms("pool", Sq[:ns, 0:1], 0.0, [bSq])
                act(JUNK2[r][:ns, :], pr, ACTF.Square, [bP[rb], bP[rb + 1]], [bJ2[r], bSq], accum=Sq[:ns, 0:1])
                act(Sq[:ns, 1:2], Sq[:ns, 0:1], ACTF.Ln, [bSq], [bSq], bias=EPS, scale=1.0 / D)
                act(Sq[:ns, 1:2], Sq[:ns, 1:2], ACTF.Exp, [bSq], [bSq], scale=-0.5)
                if kind == "s":
                    stt("dve", VF[:ns, :], pr, Sq[:ns, 1:2], DVG[:ns, :], ALU.mult, ALU.mult, [bP[rb], bP[rb + 1], bSq, bL3C], [bVF])
                    cp("pool", VT[:ns, j, :], VF[:ns, :], [bVF], [bVT])
                    dma(O["gvs"][s, s0:s0 + ns, :], VF[:ns, :], reads=[bVF], is_output=True)
                else:
                    stt("dve", VT[:ns, j, :], pr, Sq[:ns, 1:2], DVG[:ns, :], ALU.mult, ALU.mult,
                        [bP[rb], bP[rb + 1], bSq, bL3C], [bVT])
            S.barrier()
            order = []
            for c in range(NCH):
                order += [c, 16 + c]
            assert [8 + q for q in range(8)] + order == ORD["d_in"]
            ws = plan_stream("d_in", 16, first_chunk=8)
            mmbanks[0] = [0, 1, 2, 3, 4, 5]
            wi = 0
            ui = 0
            for c in range(NCH):
                g = c // 2
                wu, bwu = ws.get(wi)
                wg, bwg = ws.get(wi + 1)
                wi += 2
                for (t0, _) in TT:
                    sl = ui % 2
                    ui += 1
                    mb_ = 6 + sl
                    bku = next_mm_bank()
                    inproj(wu, bwu, t0, bku)
                    bkg = next_mm_bank()
                    inproj(wg, bwg, t0, bkg)
                    act(SGC[sl][:, :n], bank(bkg, n), ACTF.Silu, [bP[bkg]], [bSGC[sl]])
                    tt("dve", UC[sl][:, :n], bank(bku, n), SGC[sl][:, :n], ALU.mult, [bP[bku], bSGC[sl]], [bUC[sl]])
                    nsub = (n + ns - 1) // ns
                    for jj in range(nsub):
                        j = t0 // 128 + jj
                        mm(bank(mb_, n)[:, jj * ns:(jj + 1) * ns], VT[:ns, j, c * 128:(c + 1) * 128], WSTT[:ns, g, 0:ns], True, True,
                           [bVT, bC], [bP[mb_]])
                    if nsub > 1:
                        bsb = BS[:, g, :].unsqueeze(1).broadcast_to([128, nsub, 128])
                        pin = bank(mb_, n).rearrange("p (j t) -> p j t", j=nsub)
                        tin = SGC[sl][:, :n].rearrange("p (j t) -> p j t", j=nsub)
                        tt("dve", tin, pin, bsb, ALU.add, [bP[mb_], bL3C], [bSGC[sl]])
                    else:
                        tt("dve", SGC[sl][:, :n], bank(mb_, n), BS[:, g, 0:n], ALU.add, [bP[mb_], bL3C], [bSGC[sl]])
                    tt("pool", MB[:, c, t0:t0 + n], SGC[sl][:, :n], UC[sl][:, :n], ALU.mult, [bSGC[sl], bUC[sl]], [bM[c]])
            outproj("d_out", lambda c, t0: MB[:, c, t0:t0 + n], lambda c: bM[c])
            S.barrier()

        LAY = [layer0, layer1, layer2, layer3]
        for li in layers:
            LAY[li]()

        if do_final:
            cvf = Carver()
            SQ = cvf.bf16([128, NCH, 512])
            XN2 = [cvf.f32([128, NCH, 512]) for _ in range(2)]
            RSB = [RS, cvf.f32([128, 512])]
            OST = [cvf.f32([128, D]) for _ in range(2)]
            bSQ = B("SQf")
            bXN2 = [B("XN0"), B("XN1")]
            bRSB = [bRS, B("RSf1")]
            bOST = [B("OST0"), B("OST1")]
            yo = O["yp"] if kind == "p" else O["ys"]
            jjc = [0]

            def fstage1(i):
                t0 = TT[i][0]
                sl = i % 2
                for c in range(NCH):
                    act(SQ[:, c, :n], X[:, c, t0:t0 + n], ACTF.Square, [bX[c]], [bSQ])
                for c in range(NCH):
                    mm(bank(4 + sl, n), ONESM, SQ[:, c, :n], c == 0, c == NCH - 1, [bC, bSQ], [bP[4 + sl]])
                rstd_from(4 + sl, n, RSB[sl][:, :n], bRSB[sl])
                for c in range(NCH):
                    stt("dve", XN2[sl][:, c, :n], X[:, c, t0:t0 + n], FG[:, c:c + 1], RSB[sl][:, :n], ALU.mult, ALU.mult,
                        [bX[c], bC, bRSB[sl]], [bXN2[sl]])

            def fstage2(i):
                t0 = TT[i][0]
                sl = i % 2
                nsub = (n + ns - 1) // ns
                for j in range(nsub):
                    s0 = t0 + j * ns
                    o = jjc[0] % 2
                    jjc[0] += 1
                    pr = 2 if o == 0 else 6
                    for c in range(NCH):
                        tr(PS[:ns, pr * 512 + c * 128: pr * 512 + (c + 1) * 128], XN2[sl][:, c, j * ns:(j + 1) * ns], IDF, [bXN2[sl], bC],
                           [bP[pr], bP[pr + 1]])
                    cp("act", OST[o][:ns, :], PS[:ns, pr * 512:pr * 512 + 1024], [bP[pr], bP[pr + 1]], [bOST[o]])
                    dma(yo[s, s0:s0 + ns, :], OST[o][:ns, :], reads=[bOST[o]], is_output=True)

            for i in range(len(TT) + 1):
                if i < len(TT):
                    fstage1(i)
                if i >= 1:
                    fstage2(i - 1)
            S.barrier()

    for s in range(NP):
        segment("p", s)
    for s in range(NS):
        segment("s", s)

    S.emit(st)
    st.close()
    return nc


NCORES = 8
_cache = {}


def kernel(x_prompt, x_sample, cache_pool, cache_conv, cache_sb_k, cache_sb_v, norm_g, final_g,
           a_w_in, a_w_grp, a_scale, a_w_out, b_w_in, b_w_dw, b_b_dw, b_ln_g, b_ln_b, b_w_out,
           c_w_in, c_q_g, c_k_g, c_w_out, d_w_in, d_v_g, d_w_s, d_b_s, d_w_out):
    f = lambda a: np.ascontiguousarray(np.asarray(a, dtype=np.float32))
    B, T, _ = x_prompt.shape
    BS_, TS, _ = x_sample.shape
    PAST = cache_sb_k.shape[2]
    NP = B // NCORES
    NS = BS_ // NCORES
    key = (NP, T, NS, TS, PAST)
    if key not in _cache:
        _cache[key] = build(NP, T, NS, TS, PAST)
    nc = _cache[key]
    cb, cf = host_consts()
    shared = dict(norm_g=f(norm_g), final_g=f(final_g), a_w_in=f(a_w_in[0]), a_w_grp=f(a_w_grp[0]), a_scale=f(a_scale[0]),
                  a_w_out=f(a_w_out[0]), b_w_in=f(b_w_in[0]), b_w_dw=f(b_w_dw[0]), b_b_dw=f(b_b_dw[0]), b_ln_g=f(b_ln_g[0]),
                  b_ln_b=f(b_ln_b[0]), b_w_out=f(b_w_out[0]), c_w_in=f(c_w_in[0]), c_q_g=f(c_q_g[0]), c_k_g=f(c_k_g[0]),
                  c_w_out=f(c_w_out[0]), d_w_in=f(d_w_in[0]), d_v_g=f(d_v_g[0]), d_w_s=f(d_w_s[0]), d_b_s=f(d_b_s[0]),
                  d_w_out=f(d_w_out[0]), cb=cb, cf=cf)
    xp = f(x_prompt)
    xs = f(x_sample)
    cp_ = f(cache_pool[0])
    cc_ = f(cache_conv[0])
    ck_ = f(cache_sb_k[0]).reshape(BS_, PAST, D)
    cv_ = f(cache_sb_v[0]).reshape(BS_, PAST, D)
    in_maps = []
    for i in range(NCORES):
        m = dict(shared)
        m["xp"] = xp[i * NP:(i + 1) * NP]
        m["xs"] = xs[i * NS:(i + 1) * NS]
        m["cpool"] = cp_[i * NS:(i + 1) * NS]
        m["cconv"] = cc_[i * NS:(i + 1) * NS]
        m["ck"] = ck_[i * NS:(i + 1) * NS]
        m["cv"] = cv_[i * NS:(i + 1) * NS]
        in_maps.append(m)
    res = run_bass_kernel_spmd(nc, in_maps, core_ids=list(range(NCORES)))
    R = res.results
    cat = lambda k: np.concatenate([np.asarray(r[k], dtype=np.float32) for r in R], axis=0)
    yp = cat("yp")
    ys = cat("ys")
    poolp = cat("poolp")[None]
    pools = cat("pools")[None]
    convp = cat("convp")[None]
    convs = cat("convs")[None]
    kp = cat("kp").reshape(1, B, T, 8, 128)
    vp = cat("vp").reshape(1, B, T, 8, 128)
    ks = cat("ks").reshape(1, BS_, TS, 8, 128)
    vs = cat("vs").reshape(1, BS_, TS, 8, 128)
    gvs = cat("gvs")[None]
    return (yp, ys, poolp, pools, convp, convs, kp, vp, ks, vs, gvs)
```
